# Optimizing a Trainium2 kernel written in Bass

```python
import jax, jax.numpy as jnp
from jax import lax
import numpy as np

D_MODEL = 2048
BATCH = 1
SEQ = 8192
DEPTH = 1

N_META = 16
BLOCK = 128
PAD_LEN = BLOCK - N_META
MLA_HEADS = 16
Q_LORA_RANK = 1536
KV_LORA_RANK = 512
QK_NOPE_DIM = 128
QK_ROPE_DIM = 64
QK_HEAD_DIM = QK_NOPE_DIM + QK_ROPE_DIM
V_HEAD_DIM = 128
MLA_WIDTH = MLA_HEADS * V_HEAD_DIM
ROPE_THETA = 10000.0
HGRN_HEADS = 16
HGRN_EXPAND = 128
HGRN_KEY_WIDTH = HGRN_HEADS * HGRN_EXPAND
HGRN_V_DIM = D_MODEL // HGRN_HEADS
HGRN_V_WIDTH = HGRN_HEADS * HGRN_V_DIM
D_FF = 5632
CONV_WIDTH = 3
NORM_EPS = 1e-6
IN_WIDTHS = (Q_LORA_RANK, KV_LORA_RANK, QK_ROPE_DIM,
             HGRN_KEY_WIDTH, HGRN_KEY_WIDTH, HGRN_V_WIDTH,
             HGRN_V_WIDTH,
             D_MODEL, D_MODEL)
IN_COLS = sum(IN_WIDTHS)

kernel_name = "hybrid_mla_hgrn2_convffn_block"


def _rms_norm(x, g):
    xf = x.astype(jnp.float32)
    y = xf * lax.rsqrt(jnp.mean(xf * xf, axis=-1, keepdims=True) + NORM_EPS)
    return (y * g.astype(jnp.float32)).astype(x.dtype)


def _rope_tables(pos):
    inv = 1.0 / (ROPE_THETA ** (jnp.arange(0, QK_ROPE_DIM, 2, dtype=jnp.float32) / QK_ROPE_DIM))
    ang = pos.astype(jnp.float32)[..., None] * inv
    ang = jnp.concatenate([ang, ang], axis=-1)
    return jnp.cos(ang), jnp.sin(ang)


def _apply_rope(x, cos, sin):
    xf = x.astype(jnp.float32)
    x1, x2 = jnp.split(xf, 2, axis=-1)
    rot = jnp.concatenate([-x2, x1], axis=-1)
    return (xf * cos + rot * sin).astype(x.dtype)


def _mla_attention(q_nope, q_rope, k_nope, k_rope, v, valid):
    B, L, H, _ = q_nope.shape
    n_blocks = L // BLOCK
    scale = QK_HEAD_DIM ** -0.5
    key_idx = jnp.arange(L)

    def one_block(i):
        start = i * BLOCK
        qn = lax.dynamic_slice_in_dim(q_nope, start, BLOCK, axis=1)
        qr = lax.dynamic_slice_in_dim(q_rope, start, BLOCK, axis=1)
        s = (jnp.einsum('bqhd,bkhd->bhqk', qn, k_nope, preferred_element_type=jnp.float32)
             + jnp.einsum('bqhr,bkr->bhqk', qr, k_rope, preferred_element_type=jnp.float32)) * scale
        q_idx = start + jnp.arange(BLOCK)
        causal = key_idx[None, :] <= q_idx[:, None]
        visible = valid[None, :] | (key_idx[None, :] == q_idx[:, None])
        s = jnp.where(causal & visible, s, -jnp.inf)
        p = jax.nn.softmax(s, axis=-1)
        return jnp.einsum('bhqk,bkhd->bqhd', p.astype(v.dtype), v)

    out = lax.map(one_block, jnp.arange(n_blocks))
    return jnp.transpose(out, (1, 0, 2, 3, 4)).reshape(B, L, H * v.shape[-1])


def _hgrn2_chunked(q, k, v, log_f):
    B, L, H, DK = q.shape
    DV = v.shape[-1]
    n_chunks = L // BLOCK

    def to_chunks(t):
        return jnp.transpose(t.reshape(B, n_chunks, BLOCK, H, t.shape[-1]), (1, 0, 3, 2, 4))

    causal = jnp.tril(jnp.ones((BLOCK, BLOCK), dtype=bool))

    def step(S, inp):
        qc, kc, vc, gc = inp
        b = jnp.cumsum(gc, axis=2)
        diff = b[:, :, :, None, :] - b[:, :, None, :, :]
        decay = jnp.exp(jnp.where(causal[:, :, None], diff, -jnp.inf))
        scores = jnp.einsum('bhtd,bhtsd,bhsd->bhts', qc, decay, kc)
        o = (jnp.einsum('bhts,bhse->bhte', scores, vc)
             + jnp.einsum('bhtd,bhde->bhte', qc * jnp.exp(b), S))
        b_last = b[:, :, -1:, :]
        S = (jnp.exp(b_last[:, :, 0, :])[..., None] * S
             + jnp.einsum('bhsd,bhse->bhde', kc * jnp.exp(b_last - b), vc))
        return S, o

    S0 = jnp.zeros((B, H, DK, DV), jnp.float32)
    _, o = lax.scan(step, S0, (to_chunks(q), to_chunks(k), to_chunks(v), to_chunks(log_f)))
    return jnp.transpose(o, (1, 0, 3, 2, 4)).reshape(B, L, H, DV)


def setup_inputs(seed: int = 0) -> dict:
    key = jax.random.key(seed)
    ks = jax.random.split(key, 24)
    f32 = jnp.float32

    def w(k, shape, fan_in):
        return jax.random.normal(k, shape, f32) * (fan_in ** -0.5)

    def gain(k, shape):
        return 1.0 + 0.02 * jax.random.normal(k, shape, f32)

    return {
        "x": jax.random.normal(ks[0], (BATCH, SEQ, D_MODEL), f32),
        "positions": jnp.broadcast_to(jnp.arange(SEQ, dtype=jnp.int32)[None, :], (BATCH, SEQ)),
        "meta_tokens": jax.random.normal(ks[1], (N_META, D_MODEL), f32),
        "w_in": w(ks[2], (DEPTH, D_MODEL, IN_COLS), D_MODEL),
        "w_q_up": w(ks[3], (DEPTH, Q_LORA_RANK, MLA_HEADS * QK_HEAD_DIM), Q_LORA_RANK),
        "w_kv_up": w(ks[4], (DEPTH, KV_LORA_RANK, MLA_HEADS * (QK_NOPE_DIM + V_HEAD_DIM)), KV_LORA_RANK),
        "w_branch_mla": w(ks[5], (DEPTH, MLA_WIDTH, D_MODEL), MLA_WIDTH),
        "w_branch_hgrn": w(ks[6], (DEPTH, HGRN_V_WIDTH, D_MODEL), HGRN_V_WIDTH),
        "w_out": w(ks[7], (DEPTH, D_MODEL, D_MODEL), D_MODEL),
        "w_ffn_in": w(ks[8], (DEPTH, D_MODEL, 2 * D_FF), D_MODEL),
        "w_ffn_out": w(ks[9], (DEPTH, D_FF, D_MODEL), D_FF),
        "conv_w": w(ks[10], (DEPTH, CONV_WIDTH, D_FF), CONV_WIDTH),
        "conv_b": 0.01 * jax.random.normal(ks[11], (DEPTH, D_FF), f32),
        "g_mix_norm": gain(ks[12], (DEPTH, D_MODEL)),
        "g_q_norm": gain(ks[13], (DEPTH, Q_LORA_RANK)),
        "g_kv_norm": gain(ks[14], (DEPTH, KV_LORA_RANK)),
        "g_hgrn_norm": gain(ks[15], (DEPTH, HGRN_V_DIM)),
        "g_ffn_norm": gain(ks[16], (DEPTH, D_MODEL)),
        "g_final_norm": gain(ks[17], (D_MODEL,)),
        "lb_raw": 1.0 + 0.1 * jax.random.normal(ks[18], (DEPTH + 1, HGRN_KEY_WIDTH), f32),
    }


def reference(x, positions, meta_tokens, w_in, w_q_up, w_kv_up, w_branch_mla, w_branch_hgrn,
              w_out, w_ffn_in, w_ffn_out, conv_w, conv_b, g_mix_norm, g_q_norm, g_kv_norm,
              g_hgrn_norm, g_ffn_norm, g_final_norm, lb_raw):
    B, S, D = x.shape
    dt = x.dtype
    prefix = PAD_LEN + N_META
    L = prefix + S

    h = jnp.concatenate([jnp.zeros((B, PAD_LEN, D), dt),
                         jnp.broadcast_to(meta_tokens.astype(dt)[None], (B, N_META, D)),
                         x], axis=1)
    valid = jnp.arange(L) >= PAD_LEN
    pos = jnp.concatenate([jnp.zeros((B, PAD_LEN), jnp.int32),
                           jnp.broadcast_to(jnp.arange(N_META, dtype=jnp.int32)[None], (B, N_META)),
                           positions.astype(jnp.int32) + N_META], axis=1)
    cos, sin = _rope_tables(pos)
    split_pts = np.cumsum(IN_WIDTHS)[:-1].tolist()
    lb_all = jnp.cumsum(jax.nn.softmax(lb_raw.astype(jnp.float32), axis=0), axis=0)

    for layer in range(DEPTH):
        u = _rms_norm(h, g_mix_norm[layer])
        proj = u @ w_in[layer]
        q_lat, kv_lat, k_rope, hq, hf, hi, hg, gate_a, gate_b = jnp.split(proj, split_pts, axis=-1)

        q = (_rms_norm(q_lat, g_q_norm[layer]) @ w_q_up[layer]).reshape(B, L, MLA_HEADS, QK_HEAD_DIM)
        q_nope, q_rope = q[..., :QK_NOPE_DIM], q[..., QK_NOPE_DIM:]
        kv = (_rms_norm(kv_lat, g_kv_norm[layer]) @ w_kv_up[layer]).reshape(
            B, L, MLA_HEADS, QK_NOPE_DIM + V_HEAD_DIM)
        k_nope, v_mla = kv[..., :QK_NOPE_DIM], kv[..., QK_NOPE_DIM:]
        q_rope = _apply_rope(q_rope, cos[:, :, None, :], sin[:, :, None, :])
        k_rope = _apply_rope(k_rope, cos, sin)
        o_mla = _mla_attention(q_nope, q_rope, k_nope, k_rope, v_mla, valid)

        lb = lb_all[layer]
        f = lb + (1.0 - lb) * jax.nn.sigmoid(hf.astype(jnp.float32))
        vmask = valid[None, :, None]
        log_f = jnp.where(vmask, jnp.log(f), 0.0)
        k_in = jnp.where(vmask, 1.0 - f, 0.0)
        q_h = jax.nn.silu(hq.astype(jnp.float32))
        o_h = _hgrn2_chunked(q_h.reshape(B, L, HGRN_HEADS, HGRN_EXPAND),
                             k_in.reshape(B, L, HGRN_HEADS, HGRN_EXPAND),
                             hi.astype(jnp.float32).reshape(B, L, HGRN_HEADS, HGRN_V_DIM),
                             log_f.reshape(B, L, HGRN_HEADS, HGRN_EXPAND))
        o_h = _rms_norm(o_h, g_hgrn_norm[layer]) * jax.nn.silu(
            hg.astype(jnp.float32).reshape(B, L, HGRN_HEADS, HGRN_V_DIM))
        o_hgrn = o_h.reshape(B, L, HGRN_V_WIDTH).astype(dt)

        merged = (jax.nn.sigmoid(gate_a) * (o_mla @ w_branch_mla[layer])
                  + jax.nn.sigmoid(gate_b) * (o_hgrn @ w_branch_hgrn[layer]))
        h = h + merged @ w_out[layer]

        u = _rms_norm(h, g_ffn_norm[layer])
        gate, up = jnp.split(u @ w_ffn_in[layer], 2, axis=-1)
        gate = jnp.where(vmask, gate, 0.0)
        gp = jnp.pad(gate, ((0, 0), (CONV_WIDTH - 1, 0), (0, 0)))
        cw = conv_w[layer]
        conv = (cw[0] * gp[:, :-2] + cw[1] * gp[:, 1:-1] + cw[2] * gp[:, 2:]) + conv_b[layer]
        h = h + (jax.nn.silu(conv) * up) @ w_ffn_out[layer]

    out = _rms_norm(h, g_final_norm)
    return out[:, prefix:, :]
```

```python
import math
import numpy as np
import ml_dtypes
import concourse.bass as bass
import concourse.mybir as mybir
from concourse.bass_utils import run_bass_kernel_spmd

F32 = mybir.dt.float32
BF16 = mybir.dt.bfloat16
I32 = mybir.dt.int32
AF = mybir.ActivationFunctionType
ALU = mybir.AluOpType

NCORES = 8
D = 2048
KC = 16
SEQ = 8192
L = SEQ + 128
NBLK = L // 128
T = 512
QL = 1536
QLC = 12
KVL = 512
DFF = 5632
FC = 44
EPS = 1e-6
OWN = 1024
HALO = 2
NT2 = OWN + HALO
SCALE = 192 ** -0.5
PI = math.pi


class Ev:
    __slots__ = ("sid", "val", "know")

    def __init__(self, sid, val, know):
        self.sid = sid
        self.val = val
        self.know = know


class Buf:
    __slots__ = ("name", "w", "r")

    def __init__(self, name=""):
        self.name = name
        self.w = None
        self.r = []


class _Rec:
    def __init__(self):
        self.call = None

    def __getattr__(self, name):
        def f(*a, **k):
            self.call = (name, a, k)
            return None
        return f


def _record(fn):
    r = _Rec()
    fn(r)
    assert r.call is not None
    return r.call


class Prog:
    ENGS = ("pe", "act", "dve", "pool", "sp")

    def __init__(self, nc, n_dma_sems=12):
        self.nc = nc
        self.lists = {e: [] for e in self.ENGS}
        self.know = {e: {} for e in self.ENGS}
        self.sems = {}
        self.esem = {}
        self.ecnt = {}
        self._ctx = []
        for e in ("pe", "act", "dve", "pool"):
            s = self._sem("s_" + e)
            self.esem[e] = s
            self.ecnt[e] = 0
        self.dsems = [self._sem("d%d" % i) for i in range(n_dma_sems)]
        self.dcnt = [0] * n_dma_sems
        self.dlast = [None] * n_dma_sems
        self.dnext = 0
        self.fence = []
        self.fence_seen = {e: 0 for e in self.ENGS}
        self.nfence = 0
        self.ninst = 0

    def _sem(self, name):
        cm = self.nc.semaphore(name)
        s = cm.__enter__()
        self._ctx.append(cm)
        sid = len(self.sems)
        self.sems[sid] = s
        return sid

    def _wait_for(self, eng, deps):
        know = self.know[eng]
        waits = {}
        for ev in deps:
            if ev is None:
                continue
            if know.get(ev.sid, 0) >= ev.val:
                continue
            if eng == "pe" and ev.sid == self.esem["pe"]:
                continue
            if waits.get(ev.sid, (0, None))[0] < ev.val:
                waits[ev.sid] = (ev.val, ev)
        for sid, (val, ev) in waits.items():
            self.lists[eng].append(("wait", sid, val))
            for k, v in ev.know.items():
                if know.get(k, 0) < v:
                    know[k] = v
            if know.get(sid, 0) < val:
                know[sid] = val

    def _deps(self, eng, reads, writes):
        deps = []
        if self.fence_seen[eng] < self.nfence:
            deps.extend(self.fence)
            self.fence_seen[eng] = self.nfence
        for b in reads:
            deps.append(b.w)
        for b in writes:
            deps.append(b.w)
            deps.extend(b.r)
        return deps

    def op(self, eng, fn, reads=(), writes=(), signal=True):
        self._wait_for(eng, self._deps(eng, reads, writes))
        sid = self.esem[eng]
        if signal:
            self.ecnt[eng] += 1
            val = self.ecnt[eng]
        else:
            val = self.ecnt[eng] + 1
        kn = dict(self.know[eng])
        kn[sid] = max(kn.get(sid, 0), val) if eng == "pe" else kn.get(sid, 0)
        ev = Ev(sid, val, kn)
        ev.know[sid] = max(ev.know.get(sid, 0), val)
        self.lists[eng].append(("inst", _record(fn), sid if signal else None, 1))
        self.ninst += 1
        for b in reads:
            b.r.append(ev)
        for b in writes:
            b.w = ev
            b.r = []
        return ev

    def collective(self, fn, reads=(), writes=()):
        eng = "pool"
        self._wait_for(eng, self._deps(eng, reads, writes))
        sid = self._sem("cc%d" % len(self.sems))
        ev = Ev(sid, 1, dict(self.know[eng]))
        ev.know[sid] = 1
        self.lists[eng].append(("inst", _record(fn), sid, None))
        self._wait_for(eng, [ev])
        for b in reads:
            b.r.append(ev)
        for b in writes:
            b.w = ev
            b.r = []
        return ev

    def dma(self, eng, fn, reads=(), writes=(), late=False):
        i = self.dnext
        self.dnext = (self.dnext + 1) % len(self.dsems)
        deps = self._deps(eng, reads, writes)
        deps.append(self.dlast[i])
        self._wait_for(eng, deps)
        sid = self.dsems[i]
        self.dcnt[i] += 16
        kn = dict(self.know[eng])
        kn[sid] = self.dcnt[i]
        ev = Ev(sid, self.dcnt[i], kn)
        self.dlast[i] = ev
        self.lists[eng].append(("late", fn, sid, 16) if late else ("inst", _record(fn), sid, 16))
        self.ninst += 1
        for b in reads:
            b.r.append(ev)
        for b in writes:
            b.w = ev
            b.r = []
        return ev

    def barrier(self):
        evs = []
        for e in ("pe", "act", "dve", "pool"):
            if self.ecnt[e] > 0:
                evs.append(Ev(self.esem[e], self.ecnt[e], {self.esem[e]: self.ecnt[e]}))
        for i, ev in enumerate(self.dlast):
            if ev is not None:
                evs.append(Ev(ev.sid, ev.val, {ev.sid: ev.val}))
        self.fence = evs
        self.nfence += 1

    def wait_all(self, eng, evs):
        self._wait_for(eng, list(evs))

    def emit(self):
        nc = self.nc
        lists = self.lists
        sems = self.sems

        def replay(eng_name, e):
            for it in lists[eng_name]:
                if it[0] == "wait":
                    e.wait_ge(sems[it[1]], it[2])
                elif it[0] == "late":
                    it[1](e).then_inc(sems[it[2]], it[3])
                else:
                    ins = getattr(e, it[1][0])(*it[1][1], **it[1][2])
                    if it[2] is not None:
                        if it[3] is None:
                            ins.then_inc(sems[it[2]])
                        else:
                            ins.then_inc(sems[it[2]], it[3])

        with nc.Block() as block:
            @block.sync
            def _(e):
                replay("sp", e)

            @block.tensor
            def _(e):
                replay("pe", e)

            @block.scalar
            def _(e):
                replay("act", e)

            @block.vector
            def _(e):
                replay("dve", e)

            @block.gpsimd
            def _(e):
                replay("pool", e)
        for cm in reversed(self._ctx):
            cm.__exit__(None, None, None)


class Tl:
    def __init__(self, ap, nparts=1, name=""):
        self.ap = ap
        self.bufs = [Buf("%s.%d" % (name, i)) for i in range(nparts)]

    @property
    def b(self):
        return self.bufs[0]

    def all(self):
        return list(self.bufs)


class Ctx:
    def __init__(self):
        self.nc = bass.Bass("TRN2", target_bir_lowering=False)
        nc = self.nc
        self.P = Prog(nc)
        self.sb_lo = (nc.sbuf_base + 63) // 64 * 64
        self.sb_hi = nc.sbuf_top
        self.sb_ptr = self.sb_lo
        self.nid = 0
        self.psum = []
        for i in range(8):
            t = nc.alloc_psum_tensor("ps%d" % i, [128, 512], F32)
            self.psum.append(Tl(t, 1, "ps%d" % i))
        self.ps_rot = 0
        self.ps_override = None
        self.ps_rots = {}
        self.ps_pool = list(range(8))

    def sb(self, shape, dtype, nparts=1, name="t"):
        nbytes = int(np.prod(shape[1:])) * mybir.dt.size(dtype)
        nbytes = (nbytes + 63) // 64 * 64
        off = self.sb_ptr
        assert off + nbytes <= self.sb_hi, "SBUF overflow %s %d" % (name, off + nbytes - self.sb_hi)
        self.sb_ptr += nbytes
        self.nid += 1
        t = self.nc.alloc_sbuf_tensor_at("%s_%d" % (name, self.nid), list(shape), dtype, offset=off)
        return Tl(t, nparts, name)

    def sb_top(self, shape, dtype, nparts=1, name="t"):
        nbytes = int(np.prod(shape[1:])) * mybir.dt.size(dtype)
        nbytes = (nbytes + 63) // 64 * 64
        self.sb_hi = (self.sb_hi - nbytes) // 64 * 64
        assert self.sb_hi >= self.sb_ptr, "SBUF overflow (top) %s" % name
        self.nid += 1
        t = self.nc.alloc_sbuf_tensor_at("%s_%d" % (name, self.nid), list(shape), dtype, offset=self.sb_hi)
        return Tl(t, nparts, name)

    def mark(self):
        return self.sb_ptr

    def release(self, m):
        self.sb_ptr = m

    def ps(self):
        if self.ps_override is not None:
            pool, key = self.ps_override
            k = self.ps_rots.get(key, 0)
            self.ps_rots[key] = k + 1
            return self.psum[pool[k % len(pool)]]
        i = self.ps_pool[self.ps_rot % len(self.ps_pool)]
        self.ps_rot += 1
        return self.psum[i]

    def dram(self, name, shape, dtype, kind="Internal"):
        t = self.nc.dram_tensor(name, list(shape), dtype, kind=kind)
        return t


def build_program(debug=False, stop_after=None):
    C = Ctx()
    nc, P = C.nc, C.P

    def din(name, shape, dtype=F32):
        return nc.dram_tensor(name, list(shape), dtype, kind="ExternalInput").ap()

    hfull = din("hfull", [L, D])
    xown = din("xown", [NT2, D])
    posraw = din("posraw", [1, L], I32)
    posoff = din("posoff", [1, L])
    cvec = din("cvec", [128, 8])
    tri_d = din("tri", [128, 128])
    idn_d = din("idn", [128, 128])
    w_h = din("w_h", [D, 1024])
    w_q = din("w_q", [D, QL])
    w_kv = din("w_kv", [D, 640])
    wq_up = din("wq_up", [QL, 512])
    wkv_up = din("wkv_up", [KVL, 512])
    w_g = din("w_g", [D, 4096])
    wa = din("wa", [D, D])
    wb = din("wb", [D, D])
    wo = din("wo", [D, D])
    wfi = din("wfi", [D, 2 * DFF])
    wfo = din("wfo", [DFF, D])
    g_mix = din("g_mix", [1, D])
    g_ffn = din("g_ffn", [1, D])
    g_fin = din("g_fin", [1, D])
    g_q = din("g_q", [128, QLC])
    g_kv = din("g_kv", [128, 4])
    g_hn = din("g_hn", [128, 1])
    lbr = din("lbr", [128, 4])
    cw = din("cw", [128, FC * 3])
    cb = din("cb", [128, FC])
    out_d = nc.dram_tensor("out", [OWN, D], F32, kind="ExternalOutput").ap()

    rope_t = nc.dram_tensor("rope_tab", [2, 64, L], F32).ap()
    qs_t = nc.dram_tensor("q_scratch", [2, 192, L], BF16).ap()
    xin_h = nc.dram_tensor("xch_in", [512, L], BF16)
    xout_h = nc.dram_tensor("xch_out", [NCORES * 512, L], BF16)
    xin = xin_h.ap()
    xout = xout_h.ap()
    xinM, xinH = xin[0:256, :], xin[256:512, :]
    rope_b = Buf("rope_tab")
    qs_b = [Buf("qs%d" % i) for i in range(17)]
    xin_b = Buf("xin")
    xout_b = Buf("xout")
    xinH_b = xinM_b = xin_b

    dbg = {}
    if debug:
        dbg["oh"] = nc.dram_tensor("dbg_oh", [256, L], BF16, kind="ExternalOutput").ap()
        dbg["om"] = nc.dram_tensor("dbg_om", [256, L], BF16, kind="ExternalOutput").ap()
        dbg["h1"] = nc.dram_tensor("dbg_h1", [OWN, D], F32, kind="ExternalOutput").ap()

    def dump(name, tl, ap, shape, dtype):
        if not debug:
            return
        t = nc.dram_tensor("dbg_" + name, list(shape), dtype, kind="ExternalOutput").ap()
        P.dma("sp", lambda e: e.dma_start(out=t[tuple(slice(None) for _ in shape)], in_=ap), reads=[tl.b], writes=[])

    idn = C.sb([128, 128], BF16, name="idn")
    tri = C.sb([128, 128], BF16, name="tri")
    ones = C.sb([128, 128], BF16, name="ones")
    cv = C.sb([128, 8], F32, name="cv")
    gmix = C.sb([128, D], F32, name="gmix")
    P.dma("pool", lambda e: e.dma_start(out=idn.ap[:], in_=idn_d[:, :]), writes=[idn.b])
    P.dma("pool", lambda e: e.dma_start(out=tri.ap[:], in_=tri_d[:, :]), writes=[tri.b])
    P.dma("sp", lambda e: e.dma_start(out=cv.ap[:], in_=cvec[:, :]), writes=[cv.b])
    P.dma("sp", lambda e: e.dma_start(out=gmix.ap[:], in_=g_mix.partition_broadcast(128)), writes=[gmix.b])
    P.op("dve", lambda e: e.memset(ones.ap[:], 1.0), writes=[ones.b])
    base_mark = C.mark()

    tiles = [(t0, min(T, L - t0)) for t0 in range(0, L, T)]

    alt = [0]

    def evac_engine():
        alt[0] ^= 1
        return "act" if alt[0] else "dve"

    def copy_op(eng, out_ap, in_ap, reads, writes):
        if eng == "act":
            return P.op("act", lambda e: e.copy(out=out_ap, in_=in_ap), reads=reads, writes=writes)
        return P.op(eng, lambda e: e.tensor_copy(out=out_ap, in_=in_ap), reads=reads, writes=writes)

    def load_w(dst, src_ap, kc):
        P.dma("pool", lambda e: e.dma_start(out=dst.ap[:], in_=src_ap.rearrange("(k p) n -> p k n", p=128)),
              writes=dst.all())

    def rstd_from(ps_ap, n_feat, out_t, tmp_t, reads, shape_ap=None):
        P.op("act", lambda e: e.activation(out=tmp_t.ap[shape_ap] if shape_ap else tmp_t.ap[:], in_=ps_ap,
                                           func=AF.Ln, scale=1.0 / n_feat, bias=epsb.ap[:, 0:1]),
             reads=reads + [epsb.b], writes=[tmp_t.b])
        P.op("act", lambda e: e.activation(out=out_t.ap[shape_ap] if shape_ap else out_t.ap[:],
                                           in_=tmp_t.ap[shape_ap] if shape_ap else tmp_t.ap[:],
                                           func=AF.Exp, scale=-0.5),
             reads=[tmp_t.b], writes=[out_t.b])

    epsb = C.sb([128, 1], F32, name="epsb")
    P.op("dve", lambda e: e.memset(epsb.ap[:], EPS), writes=[epsb.b])
    base_mark = C.mark()

    def norm_block(src_ap, rows, xt, ut, gt, st):
        P.dma("sp", lambda e: e.dma_start(out=xt.ap[0:rows, :], in_=src_ap), writes=[xt.b])
        P.op("dve", lambda e: e.memset(st["ssq"].ap[0:rows, :], 0.0), writes=[st["ssq"].b])
        P.op("act", lambda e: e.activation(out=st["junk"].ap[0:rows, :], in_=xt.ap[0:rows, :], func=AF.Square,
                                           accum_out=st["ssq"].ap[0:rows, :]),
             reads=[xt.b], writes=[st["junk"].b, st["ssq"].b])
        P.op("act", lambda e: e.activation(out=st["t1"].ap[0:rows, :], in_=st["ssq"].ap[0:rows, :], func=AF.Ln,
                                           scale=1.0 / D, bias=epsb.ap[0:rows, 0:1]),
             reads=[st["ssq"].b, epsb.b], writes=[st["t1"].b])
        P.op("act", lambda e: e.activation(out=st["rs"].ap[0:rows, :], in_=st["t1"].ap[0:rows, :], func=AF.Exp,
                                           scale=-0.5),
             reads=[st["t1"].b], writes=[st["rs"].b])
        P.op("dve", lambda e: e.scalar_tensor_tensor(out=ut.ap[0:rows, :], in0=xt.ap[0:rows, :],
                                                     scalar=st["rs"].ap[0:rows, 0:1], in1=gt.ap[0:rows, :],
                                                     op0=ALU.mult, op1=ALU.mult),
             reads=[xt.b, st["rs"].b, gt.b], writes=[ut.b])

    def transpose_block(ut, rows, dstT, col0):
        for g4 in range(4):
            ps = C.ps()
            for j in range(4):
                kc = g4 * 4 + j
                P.op("pe", (lambda kc=kc, j=j, ps=ps: lambda e: e.matmul(
                    ps.ap[:, j * 128:j * 128 + rows], lhsT=ut.ap[0:rows, kc * 128:(kc + 1) * 128],
                    rhs=idn.ap[0:rows, 0:rows], start=True, stop=True))(),
                     reads=[ut.b, idn.b], writes=[ps.b], signal=(j == 3))
            eng = evac_engine()
            src = ps.ap[:, :].rearrange("p (j n) -> p j n", j=4)[:, :, 0:rows]
            dst = dstT.ap[:, g4 * 4:(g4 + 1) * 4, col0:col0 + rows]
            copy_op(eng, dst, src, [ps.b], [dstT.bufs[0]])

    def norm_tile(src, t0, n, uT, xts, uts, st, gt, blk_counter):
        for bi in range(n // 128):
            k = blk_counter[0] % 2
            blk_counter[0] += 1
            norm_block(src[t0 + bi * 128:t0 + (bi + 1) * 128, :], 128, xts[k], uts[k], gt, st[k])
            transpose_block(uts[k], 128, uT, bi * 128)

    def mk_norm_scratch(n=2, with_uts=True):
        xts = [C.sb([128, D], F32, name="xt") for _ in range(n)]
        uts = [C.sb([128, D], BF16, name="ut") for _ in range(n)] if with_uts else None
        junk = C.sb([128, D], BF16, name="junk")
        st = [dict(junk=junk, ssq=C.sb([128, 1], F32, name="ssq"),
                   t1=C.sb([128, 1], F32, name="t1"), rs=C.sb([128, 1], F32, name="rs")) for _ in range(n)]
        return xts, uts, st

    def proj_fm(wt, c0, m, rhsT, kcn, n, ps, pcols=0, rc0=0):
        for k in range(kcn):
            P.op("pe", (lambda k=k: lambda e: e.matmul(ps.ap[0:m, pcols:pcols + n], lhsT=wt.ap[:, k, c0:c0 + m],
                                                       rhs=rhsT.ap[:, k, rc0:rc0 + n], start=(k == 0),
                                                       stop=(k == kcn - 1)))(),
                 reads=[wt.b, rhsT.b], writes=[ps.b], signal=(k == kcn - 1))

    onec = C.sb([128, 1], F32, name="onec")
    P.op("dve", lambda e: e.memset(onec.ap[:], 1.0), writes=[onec.b])
    base_mark = C.mark()

    def sigmoid_like(ps_ap, shape_sl, e_t, reads):
        P.op("act", lambda e: e.activation(out=e_t.ap[shape_sl], in_=ps_ap, func=AF.Exp, scale=-1.0),
             reads=reads, writes=[e_t.b])
        P.op("act", lambda e: e.activation(out=e_t.ap[shape_sl], in_=e_t.ap[shape_sl], func=AF.Ln, bias=onec.ap[:, 0:1]),
             reads=[e_t.b, onec.b], writes=[e_t.b])
        P.op("act", lambda e: e.activation(out=e_t.ap[shape_sl], in_=e_t.ap[shape_sl], func=AF.Exp, scale=-1.0),
             reads=[e_t.b], writes=[e_t.b])

    m0 = C.mark()
    pi_t = C.sb([64, T], I32, name="pi")
    pf_t = C.sb([64, T], F32, name="pf")
    po_t = C.sb([64, T], F32, name="po")
    an_t = C.sb([64, T], F32, name="an")
    cs_t = C.sb([64, 2, T], F32, name="cs")
    for (t0, n) in tiles:
        P.dma("sp", lambda e, t0=t0, n=n: e.dma_start(out=pi_t.ap[:, 0:n],
                                                      in_=posraw[:, t0:t0 + n].partition_broadcast(64)),
              writes=[pi_t.b])
        P.dma("sp", lambda e, t0=t0, n=n: e.dma_start(out=po_t.ap[:, 0:n],
                                                      in_=posoff[:, t0:t0 + n].partition_broadcast(64)),
              writes=[po_t.b])
        P.op("dve", lambda e, n=n: e.tensor_copy(out=pf_t.ap[:, 0:n], in_=pi_t.ap[:, 0:n]),
             reads=[pi_t.b], writes=[pf_t.b])
        P.op("dve", lambda e, n=n: e.tensor_add(out=pf_t.ap[:, 0:n], in0=pf_t.ap[:, 0:n], in1=po_t.ap[:, 0:n]),
             reads=[pf_t.b, po_t.b], writes=[pf_t.b])
        P.op("dve", lambda e, n=n: e.tensor_scalar(out=an_t.ap[:, 0:n], in0=pf_t.ap[:, 0:n],
                                                   scalar1=cv.ap[0:64, 0:1], scalar2=None, op0=ALU.mult),
             reads=[pf_t.b, cv.b], writes=[an_t.b])
        for which, shift in ((0, 0.5 * PI), (1, 0.0)):
            P.op("dve", lambda e, n=n, shift=shift: e.tensor_scalar_add(out=pf_t.ap[:, 0:n], in0=an_t.ap[:, 0:n],
                                                                        scalar1=shift),
                 reads=[an_t.b], writes=[pf_t.b])
            P.op("dve", lambda e, n=n: e.tensor_scalar(out=po_t.ap[:, 0:n], in0=pf_t.ap[:, 0:n],
                                                       scalar1=1.0 / (2 * PI), scalar2=None, op0=ALU.mult),
                 reads=[pf_t.b], writes=[po_t.b])
            P.op("dve", lambda e, n=n: e.tensor_scalar_add(out=po_t.ap[:, 0:n], in0=po_t.ap[:, 0:n],
                                                           scalar1=12582912.0),
                 reads=[po_t.b], writes=[po_t.b])
            P.op("dve", lambda e, n=n: e.tensor_scalar_add(out=po_t.ap[:, 0:n], in0=po_t.ap[:, 0:n],
                                                           scalar1=-12582912.0),
                 reads=[po_t.b], writes=[po_t.b])
            P.op("dve", lambda e, n=n: e.scalar_tensor_tensor(out=pf_t.ap[:, 0:n], in0=po_t.ap[:, 0:n],
                                                              scalar=-6.28125, in1=pf_t.ap[:, 0:n],
                                                              op0=ALU.mult, op1=ALU.add),
                 reads=[po_t.b, pf_t.b], writes=[pf_t.b])
            P.op("dve", lambda e, n=n: e.scalar_tensor_tensor(out=pf_t.ap[:, 0:n], in0=po_t.ap[:, 0:n],
                                                              scalar=-(2 * PI - 6.28125), in1=pf_t.ap[:, 0:n],
                                                              op0=ALU.mult, op1=ALU.add),
                 reads=[po_t.b, pf_t.b], writes=[pf_t.b])
            P.op("dve", lambda e, n=n: e.tensor_scalar_min(out=pf_t.ap[:, 0:n], in0=pf_t.ap[:, 0:n],
                                                           scalar1=PI * (1 - 1e-6)),
                 reads=[pf_t.b], writes=[pf_t.b])
            P.op("dve", lambda e, n=n: e.tensor_scalar_max(out=pf_t.ap[:, 0:n], in0=pf_t.ap[:, 0:n],
                                                           scalar1=-PI * (1 - 1e-6)),
                 reads=[pf_t.b], writes=[pf_t.b])
            P.op("act", lambda e, n=n, which=which: e.activation(out=cs_t.ap[:, which, 0:n], in_=pf_t.ap[:, 0:n],
                                                                 func=AF.Sin),
                 reads=[pf_t.b], writes=[cs_t.b])
        P.op("dve", lambda e, n=n: e.tensor_scalar(out=cs_t.ap[:, 1, 0:n], in0=cs_t.ap[:, 1, 0:n],
                                                   scalar1=cv.ap[0:64, 1:2], scalar2=None, op0=ALU.mult),
             reads=[cs_t.b, cv.b], writes=[cs_t.b])
        P.dma("sp", lambda e, t0=t0, n=n: e.dma_start(out=rope_t[:, :, t0:t0 + n].rearrange("c p n -> p c n"),
                                                      in_=cs_t.ap[:, :, 0:n]),
              reads=[cs_t.b], writes=[rope_b])
    C.release(m0)
    P.barrier()

    def rope_apply(ps_a, ps_b, rows_rstd, cs, n, out_ap, tmp, scale, reads_extra, writes):
        P.op("dve", lambda e: e.tensor_tensor(out=tmp.ap[:, 0, 0:n], in0=ps_a.ap[0:64, 0:n], in1=cs.ap[:, 0, 0:n],
                                              op=ALU.mult),
             reads=[ps_a.b, cs.b], writes=[tmp.b])
        P.op("dve", lambda e: e.tensor_tensor(out=tmp.ap[:, 1, 0:n], in0=ps_b.ap[0:64, 0:n], in1=cs.ap[:, 1, 0:n],
                                              op=ALU.mult),
             reads=[ps_b.b, cs.b], writes=[tmp.b])
        P.op("dve", lambda e: e.tensor_tensor(out=tmp.ap[:, 0, 0:n], in0=tmp.ap[:, 0, 0:n], in1=tmp.ap[:, 1, 0:n],
                                              op=ALU.add),
             reads=[tmp.b], writes=[tmp.b])
        if rows_rstd is not None:
            P.op("dve", lambda e: e.scalar_tensor_tensor(out=out_ap, in0=tmp.ap[:, 0, 0:n], scalar=scale,
                                                         in1=rows_rstd.ap[0:64, 0:n], op0=ALU.mult, op1=ALU.mult),
                 reads=[tmp.b, rows_rstd.b] + reads_extra, writes=writes)
        else:
            P.op("dve", lambda e: e.tensor_copy(out=out_ap, in_=tmp.ap[:, 0, 0:n]),
                 reads=[tmp.b] + reads_extra, writes=writes)

    m0 = C.mark()
    wh = C.sb([128, KC, 1024], BF16, name="wh")
    load_w(wh, w_h, KC)
    lb_t = C.sb([128, 8], F32, name="lb")
    ghn = C.sb([128, 1], F32, name="ghn")
    P.dma("sp", lambda e: e.dma_start(out=lb_t.ap[:, 0:4], in_=lbr[:, :]), writes=[lb_t.b])
    P.dma("sp", lambda e: e.dma_start(out=ghn.ap[:], in_=g_hn[:, :]), writes=[ghn.b])
    P.op("dve", lambda e: e.tensor_sub(out=lb_t.ap[:, 4:6], in0=lb_t.ap[:, 2:4], in1=lb_t.ap[:, 0:2]),
         reads=[lb_t.b], writes=[lb_t.b])
    P.op("act", lambda e: e.activation(out=lb_t.ap[:, 4:6], in_=lb_t.ap[:, 4:6], func=AF.Exp),
         reads=[lb_t.b], writes=[lb_t.b])
    P.op("dve", lambda e: e.tensor_scalar_add(out=lb_t.ap[:, 4:6], in0=lb_t.ap[:, 4:6], scalar1=1.0),
         reads=[lb_t.b], writes=[lb_t.b])
    P.op("dve", lambda e: e.reciprocal(out=lb_t.ap[:, 4:6], in_=lb_t.ap[:, 4:6]), reads=[lb_t.b], writes=[lb_t.b])
    P.op("dve", lambda e: e.tensor_scalar(out=lb_t.ap[:, 6:8], in0=lb_t.ap[:, 4:6], scalar1=-1.0, scalar2=1.0,
                                          op0=ALU.mult, op1=ALU.add), reads=[lb_t.b], writes=[lb_t.b])
    xts, uts_unused, nst = mk_norm_scratch(with_uts=False)
    utsH = [C.sb([128, D], BF16, name="utsH") for _ in range(4)]
    uTs = [C.sb([128, KC, T], BF16, name="uT") for _ in range(2)]
    qhS = [[C.sb([128, T], F32, name="qh") for _ in range(2)] for _ in range(2)]
    ggS = [[C.sb([128, T], F32, name="gg") for _ in range(2)] for _ in range(2)]
    kkS = [[C.sb([128, T], F32, name="kk") for _ in range(2)] for _ in range(2)]
    sgS = [[C.sb([128, T], F32, name="sg") for _ in range(2)] for _ in range(2)]
    vtS = [[C.sb([128, 256], BF16, name="vt") for _ in range(4)] for _ in range(2)]
    ets = [C.sb([128, T], F32, name="et") for _ in range(3)]
    etc_ = [0]
    ohT = [C.sb([128, T], BF16, name="ohT") for _ in range(2)]
    S32 = [C.sb([128, 128], F32, name="S32") for _ in range(2)]
    Sbf = [C.sb([128, 128], BF16, name="Sbf") for _ in range(2)]
    rs = [dict(bt=C.sb([128, 128], F32, name="bt"), eb=C.sb([128, 128], F32, name="eb"),
               enb=C.sb([128, 128], F32, name="enb"), At=C.sb([128, 128], BF16, name="At"),
               Bt=C.sb([128, 128], BF16, name="Bt"), Btok=C.sb([128, 128], BF16, name="Btok"),
               sct=C.sb([128, 128], BF16, name="sct"), osq=C.sb([128, 128], BF16, name="osq"),
               rtmp=C.sb([128, 128], F32, name="rtmp"), rbc=C.sb([128, 128], F32, name="rbc"),
               ont=C.sb([128, 128], F32, name="ont")) for _ in range(8)]
    onesf = C.sb([128, 128], F32, name="onesf")
    P.op("dve", lambda e: e.memset(onesf.ap[:], 1.0), writes=[onesf.b])
    for h in range(2):
        P.op("dve", lambda e: e.memset(S32[h].ap[:], 0.0), writes=[S32[h].b])
        P.op("dve", lambda e: e.memset(Sbf[h].ap[:], 0.0), writes=[Sbf[h].b])
    nblk_ctr = [0]

    def h_norm_a(ti):
        t0, n = tiles[ti]
        for bi in range(n // 128):
            k = nblk_ctr[0] % 2
            nblk_ctr[0] += 1
            norm_block(hfull[t0 + bi * 128:t0 + (bi + 1) * 128, :], 128, xts[k], utsH[bi], gmix, nst[k])

    def h_norm_b(ti):
        t0, n = tiles[ti]
        for bi in range(n // 128):
            transpose_block(utsH[bi], 128, uTs[ti % 2], bi * 128)

    def h_proj_units(ti):
        t0, n = tiles[ti]
        st_ = ti % 2
        uT = uTs[st_]
        units = []
        for h in range(2):
            def unit_q(h=h):
                et = ets[etc_[0] % 3]
                etc_[0] += 1
                ps = C.ps()
                proj_fm(wh, h * 128, 128, uT, KC, n, ps)
                sigmoid_like(ps.ap[:, 0:n], (slice(None), slice(0, n)), et, [ps.b])
                P.op("dve", lambda e: e.tensor_tensor(out=qhS[st_][h].ap[:, 0:n], in0=ps.ap[:, 0:n], in1=et.ap[:, 0:n],
                                                      op=ALU.mult),
                     reads=[ps.b, et.b], writes=[qhS[st_][h].b])

            def unit_f(h=h):
                et = ets[etc_[0] % 3]
                etc_[0] += 1
                ps = C.ps()
                proj_fm(wh, 256 + h * 128, 128, uT, KC, n, ps)
                sigmoid_like(ps.ap[:, 0:n], (slice(None), slice(0, n)), et, [ps.b])
                P.op("dve", lambda e: e.tensor_scalar(out=et.ap[:, 0:n], in0=et.ap[:, 0:n],
                                                      scalar1=lb_t.ap[:, 6 + h:7 + h], scalar2=lb_t.ap[:, 4 + h:5 + h],
                                                      op0=ALU.mult, op1=ALU.add),
                     reads=[et.b, lb_t.b], writes=[et.b])
                P.op("act", lambda e: e.activation(out=ggS[st_][h].ap[:, 0:n], in_=et.ap[:, 0:n], func=AF.Ln),
                     reads=[et.b], writes=[ggS[st_][h].b])
                P.op("dve", lambda e: e.tensor_scalar(out=kkS[st_][h].ap[:, 0:n], in0=et.ap[:, 0:n], scalar1=-1.0,
                                                      scalar2=1.0, op0=ALU.mult, op1=ALU.add),
                     reads=[et.b], writes=[kkS[st_][h].b])
                if t0 == 0:
                    P.op("dve", lambda e: e.memset(ggS[st_][h].ap[:, 0:112], 0.0), writes=[ggS[st_][h].b])
                    P.op("dve", lambda e: e.memset(kkS[st_][h].ap[:, 0:112], 0.0), writes=[kkS[st_][h].b])

            def unit_g(h=h):
                et = ets[etc_[0] % 3]
                etc_[0] += 1
                ps = C.ps()
                proj_fm(wh, 512 + h * 128, 128, uT, KC, n, ps)
                sigmoid_like(ps.ap[:, 0:n], (slice(None), slice(0, n)), et, [ps.b])
                P.op("dve", lambda e: e.tensor_tensor(out=sgS[st_][h].ap[:, 0:n], in0=ps.ap[:, 0:n], in1=et.ap[:, 0:n],
                                                      op=ALU.mult),
                     reads=[ps.b, et.b], writes=[sgS[st_][h].b])
            units += [unit_q, unit_f, unit_g]
        for bi in range(n // 128):
            def unit_v(bi=bi):
                ps = C.ps()
                for k in range(KC):
                    P.op("pe", lambda e: e.matmul(ps.ap[:, 0:256], lhsT=uT.ap[:, k, bi * 128:(bi + 1) * 128],
                                                  rhs=wh.ap[:, k, 768:1024], start=(k == 0), stop=(k == KC - 1)),
                         reads=[uT.b, wh.b], writes=[ps.b], signal=(k == KC - 1))
                copy_op("act", vtS[st_][bi].ap[:], ps.ap[:, 0:256], [ps.b], [vtS[st_][bi].b])
            units.append(unit_v)
        return units

    h_norm_a(0)
    h_norm_b(0)
    for u_ in h_proj_units(0):
        u_()
    for ti, (t0, n) in enumerate(tiles):
        nb = n // 128
        st_ = ti % 2
        qh, gg, kk, sg, vt = qhS[st_], ggS[st_], kkS[st_], sgS[st_], vtS[st_]
        more = ti + 1 < len(tiles)
        if more:
            h_norm_a(ti + 1)
        items = [(bi, h) for bi in range(nb) for h in range(2)]
        for ii, (bi, h) in enumerate(items):
            cs = slice(bi * 128, (bi + 1) * 128)
            S_ = rs[ii]
            P.op("dve", lambda e: e.tensor_tensor_scan(out=S_["bt"].ap[:], data0=onesf.ap[:], data1=gg[h].ap[:, cs],
                                                       initial=0.0, op0=ALU.mult, op1=ALU.add),
                 reads=[onesf.b, gg[h].b], writes=[S_["bt"].b])
        for ii, (bi, h) in enumerate(items):
            S_ = rs[ii]
            P.op("act", lambda e: e.activation(out=S_["eb"].ap[:], in_=S_["bt"].ap[:], func=AF.Exp),
                 reads=[S_["bt"].b], writes=[S_["eb"].b])
            P.op("act", lambda e: e.activation(out=S_["enb"].ap[:], in_=S_["bt"].ap[:], func=AF.Exp, scale=-1.0),
                 reads=[S_["bt"].b], writes=[S_["enb"].b])
        for ii, (bi, h) in enumerate(items):
            cs = slice(bi * 128, (bi + 1) * 128)
            S_ = rs[ii]
            P.op("dve", lambda e: e.tensor_tensor(out=S_["At"].ap[:], in0=qh[h].ap[:, cs], in1=S_["eb"].ap[:], op=ALU.mult),
                 reads=[qh[h].b, S_["eb"].b], writes=[S_["At"].b])
            P.op("pool", lambda e: e.tensor_tensor(out=S_["Bt"].ap[:], in0=kk[h].ap[:, cs], in1=S_["enb"].ap[:], op=ALU.mult),
                 reads=[kk[h].b, S_["enb"].b], writes=[S_["Bt"].b])
        if more:
            h_norm_b(ti + 1)
        for ii, (bi, h) in enumerate(items):
            S_ = rs[ii]
            ps1 = C.ps()
            P.op("pe", lambda e: e.matmul(ps1.ap[:, 0:128], lhsT=S_["Bt"].ap[:], rhs=S_["At"].ap[:], start=True, stop=True),
                 reads=[S_["Bt"].b, S_["At"].b], writes=[ps1.b])
            P.op("dve", lambda e: e.tensor_tensor(out=S_["sct"].ap[:], in0=ps1.ap[:, 0:128], in1=tri.ap[:], op=ALU.mult),
                 reads=[ps1.b, tri.b], writes=[S_["sct"].b])
            ps2 = C.ps()
            P.op("pe", lambda e: e.matmul(ps2.ap[:, 0:128], lhsT=S_["Bt"].ap[:], rhs=idn.ap[:], start=True, stop=True),
                 reads=[S_["Bt"].b, idn.b], writes=[ps2.b])
            copy_op("act", S_["Btok"].ap[:], ps2.ap[:, 0:128], [ps2.b], [S_["Btok"].b])

        def stage_d(ii, ps3):
            bi, h = items[ii]
            cs = slice(bi * 128, (bi + 1) * 128)
            S_ = rs[ii]
            P.op("act", lambda e: e.activation(out=S_["osq"].ap[:], in_=ps3.ap[:, 0:128], func=AF.Square),
                 reads=[ps3.b], writes=[S_["osq"].b])
            ps5 = C.ps()
            P.op("pe", lambda e: e.matmul(ps5.ap[:, 0:128], lhsT=ones.ap[:], rhs=S_["osq"].ap[:], start=True, stop=True),
                 reads=[ones.b, S_["osq"].b], writes=[ps5.b])
            rstd_from(ps5.ap[:, 0:128], 128.0, S_["rbc"], S_["rtmp"], [ps5.b])
            P.op("dve", lambda e: e.tensor_tensor(out=S_["ont"].ap[:], in0=ps3.ap[:, 0:128], in1=S_["rbc"].ap[:], op=ALU.mult),
                 reads=[ps3.b, S_["rbc"].b], writes=[S_["ont"].b])
            P.op("dve", lambda e: e.scalar_tensor_tensor(out=ohT[h].ap[:, cs], in0=S_["ont"].ap[:], scalar=ghn.ap[:, 0:1],
                                                         in1=sg[h].ap[:, cs], op0=ALU.mult, op1=ALU.mult),
                 reads=[S_["ont"].b, ghn.b, sg[h].b], writes=[ohT[h].b])

        units = h_proj_units(ti + 1) if more else []
        prev = None
        for ii, (bi, h) in enumerate(items):
            S_ = rs[ii]
            vh = vt[bi].ap[:, h * 128:(h + 1) * 128]
            ps3 = C.ps()
            P.op("pe", lambda e: e.matmul(ps3.ap[:, 0:128], lhsT=vh, rhs=S_["sct"].ap[:], start=True, stop=False),
                 reads=[vt[bi].b, S_["sct"].b], writes=[ps3.b], signal=False)
            P.op("pe", lambda e: e.matmul(ps3.ap[:, 0:128], lhsT=Sbf[h].ap[:], rhs=S_["At"].ap[:], start=False, stop=True),
                 reads=[Sbf[h].b, S_["At"].b], writes=[ps3.b])
            ps4 = C.ps()
            P.op("pe", lambda e: e.matmul(ps4.ap[:, 0:128], lhsT=S_["Btok"].ap[:], rhs=vh, start=True, stop=True),
                 reads=[S_["Btok"].b, vt[bi].b], writes=[ps4.b])
            P.op("dve", lambda e: e.tensor_tensor(out=S32[h].ap[:], in0=ps4.ap[:, 0:128], in1=S32[h].ap[:], op=ALU.add),
                 reads=[ps4.b, S32[h].b], writes=[S32[h].b])
            P.op("dve", lambda e: e.tensor_scalar(out=S32[h].ap[:], in0=S32[h].ap[:], scalar1=S_["eb"].ap[:, 127:128],
                                                  scalar2=None, op0=ALU.mult),
                 reads=[S32[h].b, S_["eb"].b], writes=[S32[h].b])
            P.op("pool", lambda e: e.tensor_copy(out=Sbf[h].ap[:], in_=S32[h].ap[:]), reads=[S32[h].b], writes=[Sbf[h].b])
            if prev is not None:
                stage_d(*prev)
            prev = (ii, ps3)
            if units:
                units.pop(0)()
        stage_d(*prev)
        for u_ in units:
            u_()
        for h in range(2):
            P.dma("sp", lambda e: e.dma_start(out=xinH[h * 128:(h + 1) * 128, t0:t0 + n], in_=ohT[h].ap[:, 0:n]),
                  reads=[ohT[h].b], writes=[xinH_b])
            if debug:
                P.dma("sp", lambda e: e.dma_start(out=dbg["oh"][h * 128:(h + 1) * 128, t0:t0 + n], in_=ohT[h].ap[:, 0:n]),
                      reads=[ohT[h].b], writes=[])
    C.release(m0)
    P.barrier()
    if stop_after == "H":
        return C, locals()

    ACC6, ACC7 = C.psum[6], C.psum[7]
    ACCS = [(C.psum[4], C.psum[5]), (C.psum[6], C.psum[7])]
    C.ps_pool = [0, 1, 2, 3, 4, 5]

    def fold_rows(wt, gt, nch):
        for j in range(nch):
            P.op("dve", (lambda j=j: lambda e: e.tensor_scalar(out=wt.ap[:, j, :], in0=wt.ap[:, j, :],
                                                               scalar1=gt.ap[:, j:j + 1], scalar2=None,
                                                               op0=ALU.mult))(),
                 reads=[wt.b, gt.b], writes=[wt.b])

    m0 = C.mark()
    wq = C.sb([128, KC, QL], BF16, name="wq")
    load_w(wq, w_q, KC)
    wqu = C.sb([128, QLC, 512], BF16, name="wqu")
    load_w(wqu, wq_up, QLC)
    gq = C.sb([128, QLC], F32, name="gq")
    P.dma("sp", lambda e: e.dma_start(out=gq.ap[:], in_=g_q[:, :]), writes=[gq.b])
    fold_rows(wqu, gq, QLC)
    xts, uts_unused, nst = mk_norm_scratch(with_uts=False)
    utsQ = [C.sb([128, D], BF16, name="utsQ") for _ in range(4)]
    uTs = [C.sb([128, KC, T], BF16, name="uT") for _ in range(2)]
    qctr = [0]

    def q_norm_a(ti):
        t0, n = tiles[ti]
        for bi in range(n // 128):
            k = qctr[0] % 2
            qctr[0] += 1
            norm_block(hfull[t0 + bi * 128:t0 + (bi + 1) * 128, :], 128, xts[k], utsQ[bi], gmix, nst[k])

    def q_norm_b(ti):
        t0, n = tiles[ti]
        for bi in range(n // 128):
            transpose_block(utsQ[bi], 128, uTs[ti % 2], bi * 128)
    qlT = C.sb([128, QLC, T], BF16, name="qlT")
    sqt = [C.sb([128, T], BF16, name="sqt") for _ in range(3)]
    rq_tmp = C.sb([128, T], F32, name="rq_tmp")
    rq_bc = C.sb([128, T], F32, name="rq_bc")
    cst = C.sb([64, 2, T], F32, name="cst")
    rtm = C.sb([64, 2, T], F32, name="rtm")
    qn_o = [C.sb([128, T], BF16, name="qn_o") for _ in range(2)]
    qr_o = [C.sb([64, T], BF16, name="qr_o") for _ in range(2)]
    q_norm_a(0)
    q_norm_b(0)
    for ti, (t0, n) in enumerate(tiles):
        sl = (slice(None), slice(0, n))
        uT = uTs[ti % 2]
        if ti + 1 < len(tiles):
            q_norm_a(ti + 1)
        P.dma("sp", lambda e: e.dma_start(out=cst.ap[:, :, 0:n], in_=rope_t[:, :, t0:t0 + n].rearrange("c p n -> p c n")),
              reads=[rope_b], writes=[cst.b])
        pend = None
        for j in range(QLC):
            ps = C.ps()
            proj_fm(wq, j * 128, 128, uT, KC, n, ps)
            if pend is not None:
                pj, psq = pend
                P.op("pe", lambda e: e.matmul(ACC7.ap[:, 0:n], lhsT=ones.ap[:], rhs=psq.ap[:, 0:n], start=(pj == 0),
                                              stop=False),
                     reads=[ones.b, psq.b], writes=[ACC7.b])
            copy_op("act", qlT.ap[:, j, 0:n], ps.ap[:, 0:n], [ps.b], [qlT.b])
            sq = sqt[j % 3]
            P.op("act", lambda e: e.activation(out=sq.ap[:, 0:n], in_=ps.ap[:, 0:n], func=AF.Square),
                 reads=[ps.b], writes=[sq.b])
            pend = (j, sq)
        pj, psq = pend
        P.op("pe", lambda e: e.matmul(ACC7.ap[:, 0:n], lhsT=ones.ap[:], rhs=psq.ap[:, 0:n], start=False, stop=True),
             reads=[ones.b, psq.b], writes=[ACC7.b])
        if ti + 1 < len(tiles):
            q_norm_b(ti + 1)
        rstd_from(ACC7.ap[:, 0:n], float(QL), rq_bc, rq_tmp, [ACC7.b], sl)
        for h in range(2):
            ps = C.ps()
            proj_fm(wqu, h * 256, 128, qlT, QLC, n, ps)
            P.op("dve", lambda e: e.scalar_tensor_tensor(out=qn_o[h].ap[:, 0:n], in0=ps.ap[:, 0:n], scalar=SCALE,
                                                         in1=rq_bc.ap[:, 0:n], op0=ALU.mult, op1=ALU.mult),
                 reads=[ps.b, rq_bc.b], writes=[qn_o[h].b])
            psa = C.ps()
            proj_fm(wqu, h * 256 + 128, 64, qlT, QLC, n, psa)
            psb = C.ps()
            proj_fm(wqu, h * 256 + 192, 64, qlT, QLC, n, psb)
            rope_apply(psa, psb, rq_bc, cst, n, qr_o[h].ap[:, 0:n], rtm, SCALE, [], [qr_o[h].b])
            P.dma("sp", lambda e: e.dma_start(out=qs_t[h, 0:128, t0:t0 + n], in_=qn_o[h].ap[:, 0:n]),
                  reads=[qn_o[h].b], writes=[qs_b[ti]])
            P.dma("sp", lambda e: e.dma_start(out=qs_t[h, 128:192, t0:t0 + n], in_=qr_o[h].ap[:, 0:n]),
                  reads=[qr_o[h].b], writes=[qs_b[ti]])
    C.release(m0)
    P.barrier()

    m0 = C.mark()
    C.ps_pool = [0, 1, 2, 3]
    wkv = C.sb([128, KC, 640], BF16, name="wkv")
    load_w(wkv, w_kv, KC)
    wkvu = C.sb([128, 4, 512], BF16, name="wkvu")
    load_w(wkvu, wkv_up, 4)
    gkv = C.sb([128, 4], F32, name="gkv")
    P.dma("sp", lambda e: e.dma_start(out=gkv.ap[:], in_=g_kv[:, :]), writes=[gkv.b])
    fold_rows(wkvu, gkv, 4)
    knT = C.sb([128, 2, L], BF16, nparts=len(tiles), name="knT")
    krT = C.sb([64, L], BF16, nparts=len(tiles), name="krT")
    Vst = C.sb([128, NBLK, 256], BF16, nparts=len(tiles), name="Vst")
    xts, uts, nst = mk_norm_scratch()
    uT = C.sb([128, KC, T], BF16, name="uT")
    kvT = C.sb([128, 4, T], BF16, name="kvT")
    sq4 = C.sb([128, 4, T], BF16, name="sq4")
    rk_tmp = C.sb([128, T], F32, name="rk_tmp")
    rk_bc = C.sb([128, T], F32, name="rk_bc")
    rcol = C.sb([128, 4], F32, name="rcol")
    rcol_t = C.sb([128, 4], F32, name="rcol_t")
    cst = C.sb([64, 2, T], F32, name="cst")
    rtm = C.sb([64, 2, T], F32, name="rtm")
    qnS = [[C.sb([128, T], BF16, name="qn") for _ in range(2)] for _ in range(2)]
    qrS = [[C.sb([64, T], BF16, name="qr") for _ in range(2)] for _ in range(2)]
    pts = [C.sb([128, T], BF16, name="pT") for _ in range(4)]
    oo = [C.sb([128, T], BF16, name="oo") for _ in range(2)]
    rl = C.sb([128, T], F32, name="rl")
    blkc = [0]
    prot = [0]

    def kv_units(ti):
        t0, n = tiles[ti]
        sl = (slice(None), slice(0, n))
        nb = n // 128
        b0 = t0 // 128
        qn, qr = qnS[ti % 2], qrS[ti % 2]
        units = []

        def u_norm(bi):
            def f():
                k = blkc[0] % 2
                blkc[0] += 1
                norm_block(hfull[t0 + bi * 128:t0 + (bi + 1) * 128, :], 128, xts[k], uts[k], gmix, nst[k])
                transpose_block(uts[k], 128, uT, bi * 128)
            return f
        units += [u_norm(bi) for bi in range(nb)]

        def u_loads():
            P.dma("sp", lambda e: e.dma_start(out=cst.ap[:, :, 0:n], in_=rope_t[:, :, t0:t0 + n].rearrange("c p n -> p c n")),
                  reads=[rope_b], writes=[cst.b])
            for h in range(2):
                P.dma("sp", lambda e: e.dma_start(out=qn[h].ap[:, 0:n], in_=qs_t[h, 0:128, t0:t0 + n]),
                      reads=[qs_b[ti]], writes=[qn[h].b])
                P.dma("sp", lambda e: e.dma_start(out=qr[h].ap[:, 0:n], in_=qs_t[h, 128:192, t0:t0 + n]),
                      reads=[qs_b[ti]], writes=[qr[h].b])
        units.append(u_loads)

        def u_kv(j):
            def f():
                ps = C.ps()
                proj_fm(wkv, j * 128, 128, uT, KC, n, ps)
                copy_op("act", kvT.ap[:, j, 0:n], ps.ap[:, 0:n], [ps.b], [kvT.b])
                P.op("act", lambda e: e.activation(out=sq4.ap[:, j, 0:n], in_=ps.ap[:, 0:n], func=AF.Square),
                     reads=[ps.b], writes=[sq4.b])
            return f
        units += [u_kv(j) for j in range(4)]

        def u_rstd():
            ps = C.ps()
            for j in range(4):
                P.op("pe", lambda e: e.matmul(ps.ap[:, 0:n], lhsT=ones.ap[:], rhs=sq4.ap[:, j, 0:n], start=(j == 0),
                                              stop=(j == 3)),
                     reads=[ones.b, sq4.b], writes=[ps.b], signal=(j == 3))
            rstd_from(ps.ap[:, 0:n], float(KVL), rk_bc, rk_tmp, [ps.b], sl)
            for bi in range(nb):
                ps2 = C.ps()
                for j in range(4):
                    P.op("pe", lambda e: e.matmul(ps2.ap[:, 0:1], lhsT=sq4.ap[:, j, bi * 128:(bi + 1) * 128],
                                                  rhs=ones.ap[:, 0:1], start=(j == 0), stop=(j == 3)),
                         reads=[ones.b, sq4.b], writes=[ps2.b], signal=(j == 3))
                rstd_from(ps2.ap[:, 0:1], float(KVL), rcol, rcol_t, [ps2.b], (slice(None), slice(bi, bi + 1)))
        units.append(u_rstd)

        def u_krope():
            psa = C.ps()
            proj_fm(wkv, 512, 64, uT, KC, n, psa)
            psb = C.ps()
            proj_fm(wkv, 576, 64, uT, KC, n, psb)
            rope_apply(psa, psb, None, cst, n, krT.ap[:, t0:t0 + n], rtm, 1.0, [], [krT.bufs[ti]])
        units.append(u_krope)

        def u_kn(h):
            def f():
                ps = C.ps()
                proj_fm(wkvu, h * 128, 128, kvT, 4, n, ps)
                P.op("dve", lambda e: e.tensor_tensor(out=knT.ap[:, h, t0:t0 + n], in0=ps.ap[:, 0:n], in1=rk_bc.ap[:, 0:n],
                                                      op=ALU.mult),
                     reads=[ps.b, rk_bc.b], writes=[knT.bufs[ti]])
            return f
        units += [u_kn(0), u_kn(1)]

        def u_v(bi):
            def f():
                ps = C.ps()
                for j in range(4):
                    P.op("pe", lambda e: e.matmul(ps.ap[:, 0:256], lhsT=kvT.ap[:, j, bi * 128:(bi + 1) * 128],
                                                  rhs=wkvu.ap[:, j, 256:512], start=(j == 0), stop=(j == 3)),
                         reads=[kvT.b, wkvu.b], writes=[ps.b], signal=(j == 3))
                P.op("dve", lambda e: e.tensor_scalar(out=Vst.ap[:, b0 + bi, :], in0=ps.ap[:, 0:256],
                                                      scalar1=rcol.ap[:, bi:bi + 1], scalar2=None, op0=ALU.mult),
                     reads=[ps.b, rcol.b], writes=[Vst.bufs[ti]])
            return f
        units += [u_v(bi) for bi in range(nb)]
        return units

    def run_unit(u_):
        C.ps_override = ([2, 3], "kvu")
        u_()
        C.ps_override = ([0, 1], "st")

    for u_ in kv_units(0):
        run_unit(u_)
    for ti, (t0, n) in enumerate(tiles):
        sl = (slice(None), slice(0, n))
        nb = n // 128
        b0 = t0 // 128
        C.ps_override = ([0, 1], "st")
        qn, qr = qnS[ti % 2], qrS[ti % 2]
        units = kv_units(ti + 1) if ti + 1 < len(tiles) else []
        n_iter = 2 * (b0 + nb)
        every = max(1, n_iter // (len(units) + 1)) if units else 1
        it_ctr = [0]
        nbk = b0 + nb
        for h in range(2):
            AO, AL = ACCS[h]

            def st_issue(kb):
                c0 = max(0, (kb - b0) * 128)
                ncl = n - c0
                kt = kb // 4
                ps = C.ps()
                P.op("pe", lambda e: e.matmul(ps.ap[:, 0:ncl], lhsT=knT.ap[:, h, kb * 128:(kb + 1) * 128],
                                              rhs=qn[h].ap[:, c0:n], start=True, stop=False),
                     reads=[knT.bufs[kt], qn[h].b], writes=[ps.b], signal=False)
                P.op("pe", lambda e: e.matmul(ps.ap[:, 0:ncl], lhsT=krT.ap[:, kb * 128:(kb + 1) * 128],
                                              rhs=qr[h].ap[:, c0:n], start=False, stop=True),
                     reads=[krT.bufs[kt], qr[h].b], writes=[ps.b])
                return ps
            pend = st_issue(0)
            for kb in range(nbk):
                c0 = max(0, (kb - b0) * 128)
                ncl = n - c0
                ps = pend
                if kb + 1 < nbk:
                    pend = st_issue(kb + 1)
                pT = pts[prot[0] % 4]
                prot[0] += 1
                if kb == 0:
                    P.op("act", lambda e: e.activation(out=pT.ap[:, 0:ncl], in_=ps.ap[:, 0:ncl], func=AF.Exp,
                                                       bias=cv.ap[:, 2:3]),
                         reads=[ps.b, cv.b], writes=[pT.b])
                else:
                    P.op("act", lambda e: e.activation(out=pT.ap[:, 0:ncl], in_=ps.ap[:, 0:ncl], func=AF.Exp),
                         reads=[ps.b], writes=[pT.b])
                if kb >= b0:
                    P.op("pool", lambda e: e.tensor_tensor(out=pT.ap[:, 0:128], in0=pT.ap[:, 0:128], in1=tri.ap[:],
                                                           op=ALU.mult),
                         reads=[pT.b, tri.b], writes=[pT.b])
                P.op("pe", lambda e: e.matmul(AO.ap[:, c0:n], lhsT=Vst.ap[:, kb, h * 128:(h + 1) * 128],
                                              rhs=pT.ap[:, 0:ncl], start=(kb == 0), stop=(kb == nbk - 1)),
                     reads=[Vst.bufs[kb // 4], pT.b], writes=[AO.b], signal=False)
                P.op("pe", lambda e: e.matmul(AL.ap[:, c0:n], lhsT=ones.ap[:], rhs=pT.ap[:, 0:ncl],
                                              start=(kb == 0), stop=(kb == nbk - 1)),
                     reads=[ones.b, pT.b], writes=[AL.b])
                it_ctr[0] += 1
                if units and it_ctr[0] % every == 0:
                    run_unit(units.pop(0))
            P.op("dve", lambda e: e.reciprocal(out=rl.ap[:, 0:n], in_=AL.ap[:, 0:n]), reads=[AL.b], writes=[rl.b])
            P.op("dve", lambda e: e.tensor_tensor(out=oo[h].ap[:, 0:n], in0=AO.ap[:, 0:n], in1=rl.ap[:, 0:n],
                                                  op=ALU.mult),
                 reads=[AO.b, rl.b], writes=[oo[h].b])
            P.dma("sp", lambda e: e.dma_start(out=xinM[h * 128:(h + 1) * 128, t0:t0 + n], in_=oo[h].ap[:, 0:n]),
                  reads=[oo[h].b], writes=[xinM_b])
            if debug:
                P.dma("sp", lambda e: e.dma_start(out=dbg["om"][h * 128:(h + 1) * 128, t0:t0 + n], in_=oo[h].ap[:, 0:n]),
                      reads=[oo[h].b], writes=[])
        for u_ in units:
            run_unit(u_)
    C.release(m0)
    P.barrier()
    C.ps_pool = list(range(8))
    C.ps_override = None
    if stop_after == "M":
        return C, locals()

    P.collective(lambda e: e.collective_compute("AllGather", ALU.bypass, replica_groups=[list(range(NCORES))],
                                                ins=[xin_h.ap().opt()], outs=[xout_h.ap().opt()]),
                 reads=[xin_b], writes=[xout_b])

    CT = [(0, 342), (342, 342), (684, 342)]
    BLK = [(0, 2)] + [(2 + 128 * b, 128) for b in range(8)]
    sb_hi_save = C.sb_hi
    mT = C.sb_top([128, KC, NT2], BF16, name="mT")
    mk_after_mT = C.mark()
    uT2 = C.sb([128, KC, NT2], BF16, name="uT2")
    oaT = C.sb([128, KC, NT2], BF16, name="oaT")
    obT = C.sb([128, KC, NT2], BF16, name="obT")
    xloc = nc.dram_tensor("xch_loc", [NCORES * 512, NT2], BF16).ap()
    xloc_b = Buf("xloc")

    def late(e):
        pid = e.partition_id()
        return e.dma_start(out=xloc[:, :], in_=xout[:, bass.ds(pid * OWN + 126, NT2)])
    P.dma("pool", late, reads=[xout_b], writes=[xloc_b], late=True)
    for s_ in range(NCORES):
        for br in range(2):
            for hh in range(2):
                dst = (oaT if br == 0 else obT)
                r0 = s_ * 512 + br * 256 + hh * 128
                ch = s_ * 2 + hh
                P.dma("sp", lambda e: e.dma_start(out=dst.ap[:, ch, :], in_=xloc[r0:r0 + 128, :]),
                      reads=[xloc_b], writes=[dst.b])
    mk2 = C.mark()
    xts, uts, nst = mk_norm_scratch()
    for bi_, (c0, rows) in enumerate(BLK):
        k = bi_ % 2
        norm_block(xown[c0:c0 + rows, :], rows, xts[k], uts[k], gmix, nst[k])
        transpose_block(uts[k], rows, uT2, c0)
    C.release(mk2)
    P.barrier()
    wt4 = [[C.sb([128, KC, 128], BF16, name="wt4") for _ in range(2)] for _ in range(4)]
    sas = [C.sb([128, 342], F32, name="sa") for _ in range(2)]
    sbs = [C.sb([128, 342], F32, name="sb2") for _ in range(2)]
    sac = [0]
    for j in range(KC):
        w4 = [wt4[q][j % 2] for q in range(4)]
        load_w(w4[0], w_g[:, j * 128:(j + 1) * 128], KC)
        load_w(w4[1], w_g[:, D + j * 128:D + (j + 1) * 128], KC)
        load_w(w4[2], wa[:, j * 128:(j + 1) * 128], KC)
        load_w(w4[3], wb[:, j * 128:(j + 1) * 128], KC)
        for (c0, cn) in CT:
            sl = (slice(None), slice(0, cn))
            sa, sb2 = sas[sac[0] % 2], sbs[sac[0] % 2]
            sac[0] += 1
            ps = C.ps()
            proj_fm(w4[0], 0, 128, uT2, KC, cn, ps, rc0=c0)
            sigmoid_like(ps.ap[:, 0:cn], sl, sa, [ps.b])
            ps = C.ps()
            proj_fm(w4[2], 0, 128, oaT, KC, cn, ps, rc0=c0)
            P.op("dve", lambda e: e.tensor_tensor(out=sa.ap[:, 0:cn], in0=ps.ap[:, 0:cn], in1=sa.ap[:, 0:cn], op=ALU.mult),
                 reads=[ps.b, sa.b], writes=[sa.b])
            ps = C.ps()
            proj_fm(w4[1], 0, 128, uT2, KC, cn, ps, rc0=c0)
            sigmoid_like(ps.ap[:, 0:cn], sl, sb2, [ps.b])
            ps = C.ps()
            proj_fm(w4[3], 0, 128, obT, KC, cn, ps, rc0=c0)
            P.op("dve", lambda e: e.tensor_tensor(out=sb2.ap[:, 0:cn], in0=ps.ap[:, 0:cn], in1=sb2.ap[:, 0:cn], op=ALU.mult),
                 reads=[ps.b, sb2.b], writes=[sb2.b])
            P.op("dve", lambda e: e.tensor_tensor(out=mT.ap[:, j, c0:c0 + cn], in0=sa.ap[:, 0:cn], in1=sb2.ap[:, 0:cn],
                                                  op=ALU.add),
                 reads=[sa.b, sb2.b], writes=[mT.b])
    C.release(mk_after_mT)
    P.barrier()

    h1 = C.sb([128, 9, D], F32, nparts=9, name="h1")
    mk3 = C.mark()
    wo_t = [C.sb([128, KC, 512], BF16, name="wo_t") for _ in range(2)]
    for bi_, (c0, rows) in enumerate(BLK):
        P.dma("sp", lambda e: e.dma_start(out=h1.ap[0:rows, bi_, :], in_=xown[c0:c0 + rows, :]), writes=[h1.bufs[bi_]])
    for nn in range(4):
        wt = wo_t[nn % 2]
        load_w(wt, wo[:, nn * 512:(nn + 1) * 512], KC)
        for bi_, (c0, rows) in enumerate(BLK):
            ps = C.ps()
            for j in range(KC):
                P.op("pe", lambda e: e.matmul(ps.ap[0:rows, 0:512], lhsT=mT.ap[:, j, c0:c0 + rows], rhs=wt.ap[:, j, :],
                                              start=(j == 0), stop=(j == KC - 1)),
                     reads=[mT.b, wt.b], writes=[ps.b], signal=(j == KC - 1))
            P.op("dve", lambda e: e.tensor_tensor(out=h1.ap[0:rows, bi_, nn * 512:(nn + 1) * 512],
                                                  in0=ps.ap[0:rows, 0:512],
                                                  in1=h1.ap[0:rows, bi_, nn * 512:(nn + 1) * 512], op=ALU.add),
                 reads=[ps.b, h1.bufs[bi_]], writes=[h1.bufs[bi_]])
    C.release(mk3)
    C.sb_hi = sb_hi_save
    P.barrier()
    if debug:
        for b in range(8):
            P.dma("sp", lambda e: e.dma_start(out=dbg["h1"][b * 128:(b + 1) * 128, :], in_=h1.ap[:, 1 + b, :]),
                  reads=[h1.bufs[1 + b]], writes=[])

    u2T = C.sb([128, KC, NT2], BF16, name="u2T")
    gff = C.sb([128, D], F32, name="gff")
    P.dma("sp", lambda e: e.dma_start(out=gff.ap[:], in_=g_ffn.partition_broadcast(128)), writes=[gff.b])
    uts2 = [C.sb([128, D], BF16, name="uts2")] * 2
    junk2 = C.sb([128, D], BF16, name="junk2")
    st2 = [dict(ssq=C.sb([128, 1], F32, name="ssq2"), t1=C.sb([128, 1], F32, name="t12"), rs=C.sb([128, 1], F32, name="rs2"))
           for _ in range(2)]

    def norm_sb(src_ap, src_buf, rows, out_ap, out_buf, gt, st):
        P.op("dve", lambda e: e.memset(st["ssq"].ap[0:rows, :], 0.0), writes=[st["ssq"].b])
        P.op("act", lambda e: e.activation(out=junk2.ap[0:rows, :], in_=src_ap, func=AF.Square,
                                           accum_out=st["ssq"].ap[0:rows, :]),
             reads=[src_buf], writes=[junk2.b, st["ssq"].b])
        P.op("act", lambda e: e.activation(out=st["t1"].ap[0:rows, :], in_=st["ssq"].ap[0:rows, :], func=AF.Ln,
                                           scale=1.0 / D, bias=epsb.ap[0:rows, 0:1]),
             reads=[st["ssq"].b, epsb.b], writes=[st["t1"].b])
        P.op("act", lambda e: e.activation(out=st["rs"].ap[0:rows, :], in_=st["t1"].ap[0:rows, :], func=AF.Exp, scale=-0.5),
             reads=[st["t1"].b], writes=[st["rs"].b])
        P.op("dve", lambda e: e.scalar_tensor_tensor(out=out_ap, in0=src_ap, scalar=st["rs"].ap[0:rows, 0:1],
                                                     in1=gt.ap[0:rows, :], op0=ALU.mult, op1=ALU.mult),
             reads=[src_buf, st["rs"].b, gt.b], writes=[out_buf])

    for bi_, (c0, rows) in enumerate(BLK):
        k = bi_ % 2
        norm_sb(h1.ap[0:rows, bi_, :], h1.bufs[bi_], rows, uts2[k].ap[0:rows, :], uts2[k].b, gff, st2[k])
        transpose_block(uts2[k], rows, u2T, c0)

    GSZ = 11
    mk4 = C.mark()
    aT = C.sb([128, GSZ, OWN], BF16, name="aT")
    gT = C.sb([128, NT2], F32, name="gT")
    cT = C.sb([128, OWN], F32, name="cT")
    eT = C.sb([128, OWN], F32, name="eT")
    cwt = C.sb([128, FC * 3], F32, name="cwt")
    cbt = C.sb([128, FC], F32, name="cbt")
    P.dma("sp", lambda e: e.dma_start(out=cwt.ap[:], in_=cw[:, :]), writes=[cwt.b])
    P.dma("sp", lambda e: e.dma_start(out=cbt.ap[:], in_=cb[:, :]), writes=[cbt.b])
    wgu = [[C.sb([128, KC, 128], BF16, name="wgu") for _ in range(2)] for _ in range(2)]
    wfo_t = [C.sb([128, GSZ, 256], BF16, name="wfo_t") for _ in range(2)]
    wfc = [0]
    for g in range(FC // GSZ):
        for jj in range(GSZ):
            j = g * GSZ + jj
            wg_ = wgu[0][j % 2]
            wu_ = wgu[1][j % 2]
            load_w(wg_, wfi[:, j * 128:(j + 1) * 128], KC)
            load_w(wu_, wfi[:, DFF + j * 128:DFF + (j + 1) * 128], KC)
            for (c0, cn) in CT:
                ps = C.ps()
                proj_fm(wg_, 0, 128, u2T, KC, cn, ps, rc0=c0)
                copy_op("act", gT.ap[:, c0:c0 + cn], ps.ap[:, 0:cn], [ps.b], [gT.b])
            P.op("dve", lambda e: e.tensor_scalar(out=cT.ap[:], in0=gT.ap[:, 2:2 + OWN], scalar1=cwt.ap[:, 3 * j + 2:3 * j + 3],
                                                  scalar2=cbt.ap[:, j:j + 1], op0=ALU.mult, op1=ALU.add),
                 reads=[gT.b, cwt.b, cbt.b], writes=[cT.b])
            P.op("dve", lambda e: e.scalar_tensor_tensor(out=cT.ap[:], in0=gT.ap[:, 1:1 + OWN],
                                                         scalar=cwt.ap[:, 3 * j + 1:3 * j + 2], in1=cT.ap[:],
                                                         op0=ALU.mult, op1=ALU.add),
                 reads=[gT.b, cwt.b, cT.b], writes=[cT.b])
            P.op("dve", lambda e: e.scalar_tensor_tensor(out=cT.ap[:], in0=gT.ap[:, 0:OWN],
                                                         scalar=cwt.ap[:, 3 * j:3 * j + 1], in1=cT.ap[:],
                                                         op0=ALU.mult, op1=ALU.add),
                 reads=[gT.b, cwt.b, cT.b], writes=[cT.b])
            sigmoid_like(cT.ap[:], (slice(None), slice(0, OWN)), eT, [cT.b])
            P.op("dve", lambda e: e.tensor_tensor(out=cT.ap[:], in0=cT.ap[:], in1=eT.ap[:], op=ALU.mult),
                 reads=[cT.b, eT.b], writes=[cT.b])
            for hf in range(2):
                ps = C.ps()
                proj_fm(wu_, 0, 128, u2T, KC, 512, ps, rc0=2 + 512 * hf)
                P.op("dve", lambda e: e.tensor_tensor(out=aT.ap[:, jj, hf * 512:(hf + 1) * 512], in0=ps.ap[:, 0:512],
                                                      in1=cT.ap[:, hf * 512:(hf + 1) * 512], op=ALU.mult),
                     reads=[ps.b, cT.b], writes=[aT.b])
        for nn in range(8):
            wt = wfo_t[wfc[0] % 2]
            wfc[0] += 1
            load_w(wt, wfo[g * GSZ * 128:(g + 1) * GSZ * 128, nn * 256:(nn + 1) * 256], GSZ)
            for b in range(8):
                ps = C.ps()
                for jj in range(GSZ):
                    P.op("pe", lambda e: e.matmul(ps.ap[:, 0:256], lhsT=aT.ap[:, jj, b * 128:(b + 1) * 128],
                                                  rhs=wt.ap[:, jj, :], start=(jj == 0), stop=(jj == GSZ - 1)),
                         reads=[aT.b, wt.b], writes=[ps.b], signal=(jj == GSZ - 1))
                P.op("dve", lambda e: e.tensor_tensor(out=h1.ap[:, 1 + b, nn * 256:(nn + 1) * 256], in0=ps.ap[:, 0:256],
                                                      in1=h1.ap[:, 1 + b, nn * 256:(nn + 1) * 256], op=ALU.add),
                     reads=[ps.b, h1.bufs[1 + b]], writes=[h1.bufs[1 + b]])

    C.release(mk4)
    P.barrier()
    gfn = C.sb([128, D], F32, name="gfn")
    P.dma("sp", lambda e: e.dma_start(out=gfn.ap[:], in_=g_fin.partition_broadcast(128)), writes=[gfn.b])
    outt = [C.sb([128, D], F32, name="outt") for _ in range(2)]
    out_evs = []
    for b in range(8):
        k = b % 2
        norm_sb(h1.ap[:, 1 + b, :], h1.bufs[1 + b], 128, outt[k].ap[:, :], outt[k].b, gfn, st2[k])
        out_evs.append(P.dma("sp", lambda e: e.dma_start(out=out_d[b * 128:(b + 1) * 128, :], in_=outt[k].ap[:, :]),
                             reads=[outt[k].b], writes=[]))
    return C, locals()


def make_in_maps(inp):
    f32 = np.float32
    x = np.asarray(inp["x"], dtype=f32)[0]
    meta = np.asarray(inp["meta_tokens"], dtype=f32)
    hfull = np.concatenate([np.zeros((112, D), f32), meta, x], axis=0)
    pos = np.asarray(inp["positions"]).astype(np.int32)[0]
    posraw = np.concatenate([np.zeros(112, np.int32), np.arange(16, dtype=np.int32), pos])[None, :]
    posoff = np.concatenate([np.zeros(128, f32), np.full(SEQ, 16.0, f32)])[None, :]
    inv = (1.0 / (10000.0 ** (np.arange(0, 64, 2, dtype=f32) / 64.0))).astype(f32)
    cvec = np.zeros((128, 8), f32)
    cvec[0:64, 0] = np.concatenate([inv, inv])
    cvec[0:64, 1] = np.concatenate([-np.ones(32, f32), np.ones(32, f32)])
    cvec[0:112, 2] = -30000.0
    tri = np.triu(np.ones((128, 128), f32))
    idn = np.eye(128, dtype=f32)
    w_in = np.asarray(inp["w_in"], dtype=f32)[0]
    o_q, o_kv, o_kr = 0, 1536, 2048
    o_hq, o_hf, o_hi, o_hg = 2112, 2112 + 2048, 2112 + 4096, 2112 + 6144
    o_ga = 2112 + 8192
    perm = np.concatenate([np.arange(32, 64), np.arange(0, 32)])
    w_q = np.ascontiguousarray(w_in[:, o_q:o_q + 1536])
    kr = w_in[:, o_kr:o_kr + 64]
    w_kv = np.ascontiguousarray(np.concatenate([w_in[:, o_kv:o_kv + 512], kr, kr[:, perm]], axis=1))
    w_g = np.ascontiguousarray(w_in[:, o_ga:o_ga + 4096])
    wqu = np.asarray(inp["w_q_up"], dtype=f32)[0]
    wkvu = np.asarray(inp["w_kv_up"], dtype=f32)[0]
    lb_raw = np.asarray(inp["lb_raw"], dtype=f32)
    conv_w = np.asarray(inp["conv_w"], dtype=f32)[0]
    conv_b = np.asarray(inp["conv_b"], dtype=f32)[0]
    cw = np.ascontiguousarray(conv_w.reshape(3, FC, 128).transpose(2, 1, 0).reshape(128, FC * 3))
    cb = np.ascontiguousarray(conv_b.reshape(FC, 128).T)
    common = dict(
        hfull=hfull, posraw=posraw, posoff=posoff, cvec=cvec, tri=tri, idn=idn, w_q=w_q, w_kv=w_kv, w_g=w_g,
        wa=np.asarray(inp["w_branch_mla"], dtype=f32)[0], wb=np.asarray(inp["w_branch_hgrn"], dtype=f32)[0],
        wo=np.asarray(inp["w_out"], dtype=f32)[0], wfi=np.asarray(inp["w_ffn_in"], dtype=f32)[0],
        wfo=np.asarray(inp["w_ffn_out"], dtype=f32)[0],
        g_mix=np.asarray(inp["g_mix_norm"], dtype=f32)[0][None, :],
        g_ffn=np.asarray(inp["g_ffn_norm"], dtype=f32)[0][None, :],
        g_fin=np.asarray(inp["g_final_norm"], dtype=f32)[None, :],
        g_q=np.ascontiguousarray(np.asarray(inp["g_q_norm"], dtype=f32)[0].reshape(QLC, 128).T),
        g_kv=np.ascontiguousarray(np.asarray(inp["g_kv_norm"], dtype=f32)[0].reshape(4, 128).T),
        g_hn=np.asarray(inp["g_hgrn_norm"], dtype=f32)[0][:, None].copy(),
        cw=cw, cb=cb,
    )
    maps = []
    for c in range(NCORES):
        hs = [2 * c, 2 * c + 1]
        cols = []
        for base in (o_hq, o_hf, o_hg, o_hi):
            for hh in hs:
                cols.append(w_in[:, base + hh * 128: base + (hh + 1) * 128])
        w_h = np.ascontiguousarray(np.concatenate(cols, axis=1))
        qc = []
        for hh in hs:
            blk = wqu[:, hh * 192:(hh + 1) * 192]
            rope = blk[:, 128:192]
            qc += [blk[:, 0:128], rope, rope[:, perm]]
        wq_up = np.ascontiguousarray(np.concatenate(qc, axis=1))
        kc_ = [wkvu[:, hh * 256: hh * 256 + 128] for hh in hs] + [wkvu[:, hh * 256 + 128: hh * 256 + 256] for hh in hs]
        wkv_up = np.ascontiguousarray(np.concatenate(kc_, axis=1))
        lbr = np.stack([lb_raw[0, hs[0] * 128:(hs[0] + 1) * 128], lb_raw[0, hs[1] * 128:(hs[1] + 1) * 128],
                        lb_raw[1, hs[0] * 128:(hs[0] + 1) * 128], lb_raw[1, hs[1] * 128:(hs[1] + 1) * 128]], axis=1)
        m = dict(common)
        m.update(w_h=w_h, wq_up=wq_up, wkv_up=wkv_up, lbr=np.ascontiguousarray(lbr),
                 xown=np.ascontiguousarray(hfull[126 + OWN * c: 126 + OWN * c + NT2]))
        maps.append(m)
    return maps


_CACHE = {}


def _declared_inputs(nc):
    names = set()
    for alloc in nc.allocations:
        if isinstance(alloc, mybir.MemoryLocationSet) and alloc.kind == "ExternalInput":
            names.add(alloc.memorylocations[0].name)
    return names


def finish_program(C, extra_evs=()):
    P = C.P
    P.wait_all("sp", [ev for ev in P.dlast if ev is not None] + list(extra_evs))
    P.emit()
    return C.nc


def kernel(**inputs):
    C, loc = build_program(debug=False)
    nc = finish_program(C)
    names = _declared_inputs(nc)
    maps = [{k: v for k, v in m.items() if k in names} for m in make_in_maps(inputs)]
    res = run_bass_kernel_spmd(nc, maps, core_ids=list(range(NCORES)))
    out = np.concatenate([np.asarray(res.results[c]["out"], dtype=np.float32) for c in range(NCORES)], axis=0)
    return out.reshape(1, SEQ, D)
```
